# Optimizing a Trainium2 kernel written in Bass

```python
import math
import jax, jax.numpy as jnp
from jax import lax
import numpy as np

D_MODEL = 1024
BATCH = 16
SEQ = 2048
DEPTH = 2

CTX_LEN = 256
GRID_W = 64
N_MIXERS = 4
GROUP_W = D_MODEL // N_MIXERS
D_MIX = GROUP_W * N_MIXERS
HEAD_DIM = 64
HG_DK = 64
HG_DV = 64
HG_HEADS = GROUP_W // HG_DV
HG_CHUNK = 64
GA_HEADS = GROUP_W // HEAD_DIM
GA_KV = GA_HEADS // 2
DF_V = 64
DF_QK = DF_V // 2
DF_HEADS = GROUP_W // DF_V
WN_HEADS = GROUP_W // HEAD_DIM
WN_KV = WN_HEADS // 2
WINDOW = 128
Q_BLOCK = 128
ROPE_THETA = 10000.0
LN_EPS = 1e-5
RMS_EPS = 1e-6

IN_WIDTHS = (
    HG_HEADS * HG_DK, HG_HEADS * HG_DV, HG_HEADS * HG_DK, HG_HEADS * HG_DK,
    GA_HEADS * HEAD_DIM, GA_KV * HEAD_DIM, GA_KV * HEAD_DIM,
    DF_HEADS * 2 * DF_QK, DF_HEADS * 2 * DF_QK, DF_HEADS * DF_V,
    WN_HEADS * HEAD_DIM, WN_KV * HEAD_DIM, WN_KV * HEAD_DIM,
    D_MIX,
)
IN_WIDTH = sum(IN_WIDTHS)
SPLIT_AT = tuple(int(v) for v in np.cumsum(IN_WIDTHS)[:-1])

kernel_name = "hybrid_parallel_groups_dit_block"


def rms_norm(x, g):
    xf = x.astype(jnp.float32)
    y = xf * lax.rsqrt(jnp.mean(xf * xf, axis=-1, keepdims=True) + RMS_EPS)
    return (y * g).astype(x.dtype)


def layer_norm(x, g, b):
    xf = x.astype(jnp.float32)
    mu = jnp.mean(xf, axis=-1, keepdims=True)
    var = jnp.mean(jnp.square(xf - mu), axis=-1, keepdims=True)
    return ((xf - mu) * lax.rsqrt(var + LN_EPS) * g + b).astype(x.dtype)


def split_heads(t, n_heads):
    return t.reshape(t.shape[:-1] + (n_heads, t.shape[-1] // n_heads))


def merge_heads(t):
    return t.reshape(t.shape[:2] + (-1,))


def axial_rope(n_tok, dim):
    rows = n_tok // GRID_W
    row = jnp.repeat(jnp.arange(rows, dtype=jnp.float32), GRID_W)
    col = jnp.broadcast_to(jnp.arange(GRID_W, dtype=jnp.float32), (rows, GRID_W)).reshape(-1)
    d_axis = dim // 2
    inv = ROPE_THETA ** (-jnp.arange(0, d_axis, 2, dtype=jnp.float32) / d_axis)
    ang_r = row[:, None] * inv
    ang_c = col[:, None] * inv
    ang = jnp.concatenate([ang_r, ang_r, ang_c, ang_c], axis=-1)
    return jnp.cos(ang), jnp.sin(ang)


def apply_rope(x, rope):
    cos, sin = rope
    a1, a2, b1, b2 = jnp.split(x, 4, axis=-1)
    rot = jnp.concatenate([-a2, a1, -b2, b1], axis=-1)
    return (x * cos[:, None, :] + rot * sin[:, None, :]).astype(x.dtype)


def sweep_query_blocks(fn, q):
    b, s = q.shape[0], q.shape[1]
    nb = s // Q_BLOCK
    qb = jnp.moveaxis(q.reshape((b, nb, Q_BLOCK) + q.shape[2:]), 1, 0)
    out = lax.map(fn, qb)
    return jnp.moveaxis(out, 0, 1).reshape((b, s) + out.shape[3:])


def hgrn_forget(z, lb):
    f = lb + (1.0 - lb) * jax.nn.sigmoid(z.astype(jnp.float32))
    return (1.0 - f).astype(z.dtype), jnp.log(f)


def gla_chunk_scan(q, k, v, logf, s0):
    b, t, h, _ = q.shape
    dv = v.shape[-1]
    nc = t // HG_CHUNK

    def to_chunks(a):
        return jnp.moveaxis(a.reshape(b, nc, HG_CHUNK, h, a.shape[-1]), 1, 0)

    incl = jnp.tril(jnp.ones((HG_CHUNK, HG_CHUNK), dtype=bool))

    def step(s, xs):
        qc, kc, vc, fc = xs
        qf, kf, vf = qc.astype(jnp.float32), kc.astype(jnp.float32), vc.astype(jnp.float32)
        bcum = jnp.cumsum(fc, axis=1)
        o_inter = jnp.einsum('bthk,bhkv->bthv', qf * jnp.exp(bcum), s)
        diff = bcum[:, :, None] - bcum[:, None, :]
        decay = jnp.exp(jnp.where(incl[None, :, :, None, None], diff, -jnp.inf))
        att = jnp.einsum('bthk,bshk,btshk->bhts', qf, kf, decay)
        o_intra = jnp.einsum('bhts,bshv->bthv', att, vf)
        blast = bcum[:, -1]
        s_new = jnp.exp(blast)[..., None] * s + jnp.einsum(
            'bshk,bshv->bhkv', kf * jnp.exp(blast[:, None] - bcum), vf)
        return s_new, (o_inter + o_intra).astype(v.dtype)

    s_fin, o = lax.scan(step, s0, (to_chunks(q), to_chunks(k), to_chunks(v), to_chunks(logf)))
    return jnp.moveaxis(o, 0, 1).reshape(b, t, h, dv), s_fin


def gla_final_state(k, v, logf):
    after = lax.cumsum(logf, axis=1, reverse=True) - logf
    return jnp.einsum('bshk,bshv->bhkv', k.astype(jnp.float32) * jnp.exp(after), v.astype(jnp.float32))


def hgrn2_mixer(q, i, f_fwd, f_bwd, qc, ic, fc_fwd, fc_bwd, lb, norm_g, ctx_out):
    hs = lambda t: split_heads(t, HG_HEADS)
    flip = lambda t: jnp.flip(t, axis=1)
    q, qc = jax.nn.silu(hs(q)), jax.nn.silu(hs(qc))
    i, ic = hs(i), hs(ic)
    lb_f = lb[0].reshape(HG_HEADS, HG_DK)
    lb_b = lb[1].reshape(HG_HEADS, HG_DK)
    k_f, lf_f = hgrn_forget(hs(f_fwd), lb_f)
    k_b, lf_b = hgrn_forget(hs(f_bwd), lb_b)
    kc_f, lfc_f = hgrn_forget(hs(fc_fwd), lb_f)
    kc_b, lfc_b = hgrn_forget(hs(fc_bwd), lb_b)
    b = qc.shape[0]
    if ctx_out:
        zeros = jnp.zeros((b, HG_HEADS, HG_DK, HG_DV), jnp.float32)
        oc_f, s_cf = gla_chunk_scan(qc, kc_f, ic, lfc_f, zeros)
        oc_b, s_cb = gla_chunk_scan(flip(qc), flip(kc_b), flip(ic), flip(lfc_b), zeros)
        yc = merge_heads(rms_norm(oc_f + flip(oc_b), norm_g))
    else:
        s_cf = gla_final_state(kc_f, ic, lfc_f)
        s_cb = gla_final_state(flip(kc_b), flip(ic), flip(lfc_b))
        yc = None
    o_f, _ = gla_chunk_scan(q, k_f, i, lf_f, s_cf)
    o_b, _ = gla_chunk_scan(flip(q), flip(k_b), flip(i), flip(lf_b), s_cb)
    y = merge_heads(rms_norm(o_f + flip(o_b), norm_g))
    return y, yc


def global_gqa(q, k, v, sink=None):
    b, s, h, d = q.shape
    kv = k.shape[2]
    g = h // kv
    scale = d ** -0.5
    qg = q.reshape(b, s, kv, g, d)

    def block(qb):
        logits = jnp.einsum('bqkgd,btkd->bkgqt', qb, k).astype(jnp.float32) * scale
        if sink is None:
            p = jax.nn.softmax(logits, axis=-1)
        else:
            col = jnp.broadcast_to(sink.astype(jnp.float32).reshape(kv, g)[None, :, :, None, None],
                                   logits.shape[:-1] + (1,))
            p = jax.nn.softmax(jnp.concatenate([logits, col], axis=-1), axis=-1)[..., :-1]
        return jnp.einsum('bkgqt,btkd->bqkgd', p.astype(v.dtype), v)

    return sweep_query_blocks(block, qg).reshape(b, s, h, d)


def diff_attention(q, k, v, lam, lam_init, subln_g):
    scale = q.shape[-1] ** -0.5

    def block(qb):
        logits = jnp.einsum('bqhmd,bthmd->bhmqt', qb, k).astype(jnp.float32) * scale
        p = jax.nn.softmax(logits, axis=-1)
        w = p[:, :, 0] - lam * p[:, :, 1]
        return jnp.einsum('bhqt,bthd->bqhd', w.astype(v.dtype), v)

    o = sweep_query_blocks(block, q)
    return rms_norm(o, subln_g) * (1.0 - lam_init)


def window_gqa_latent(q, k, v, k_ctx, v_ctx, sink):
    b, s, h, d = q.shape
    kv = k.shape[2]
    g = h // kv
    nb = s // Q_BLOCK
    scale = d ** -0.5

    def banded(t):
        tp = jnp.pad(t, ((0, 0), (Q_BLOCK, Q_BLOCK), (0, 0), (0, 0)))
        tb = tp.reshape(b, nb + 2, Q_BLOCK, kv, d)
        return jnp.concatenate([tb[:, :-2], tb[:, 1:-1], tb[:, 2:]], axis=2)

    kb, vb = banded(k), banded(v)
    qb = q.reshape(b, nb, Q_BLOCK, kv, g, d)
    blk = jnp.arange(nb)[:, None, None]
    qpos = blk * Q_BLOCK + jnp.arange(Q_BLOCK)[None, :, None]
    kpos = (blk - 1) * Q_BLOCK + jnp.arange(3 * Q_BLOCK)[None, None, :]
    allowed = (jnp.abs(kpos - qpos) <= WINDOW) & (kpos >= 0) & (kpos < s)
    lw = jnp.einsum('bnqkgd,bnskd->bnkgqs', qb, kb).astype(jnp.float32) * scale
    lw = jnp.where(allowed[None, :, None, None], lw, -jnp.inf)
    lc = jnp.einsum('bnqkgd,bskd->bnkgqs', qb, k_ctx).astype(jnp.float32) * scale
    ls = jnp.broadcast_to(sink.astype(jnp.float32).reshape(kv, g)[None, None, :, :, None, None],
                          lw.shape[:-1] + (1,))
    p = jax.nn.softmax(jnp.concatenate([lw, lc, ls], axis=-1), axis=-1).astype(v.dtype)
    n_w = 3 * Q_BLOCK
    n_c = k_ctx.shape[1]
    o = (jnp.einsum('bnkgqs,bnskd->bnqkgd', p[..., :n_w], vb)
         + jnp.einsum('bnkgqs,bskd->bnqkgd', p[..., n_w:n_w + n_c], v_ctx))
    return o.reshape(b, s, h, d)


def mixer_layer(x, cx, c, c_ctx, w_in, w_out, w_ada, b_ada, ln_g, ln_b, lb, hg_norm_g,
                ga_qn, ga_kn, df_lam, df_subln_g, wn_sink, rope64, rope32, layer_idx, ctx_out):
    alpha = (2.0 * DEPTH) ** 0.25
    shift, scale, gate = jnp.split(jax.nn.silu(c) @ w_ada + b_ada, 3, axis=-1)
    shift_c, scale_c, gate_c = jnp.split(jax.nn.silu(c_ctx) @ w_ada + b_ada, 3, axis=-1)
    h = x * (1.0 + scale[:, None]) + shift[:, None]
    hc = cx * (1.0 + scale_c) + shift_c
    (a_q, a_i, a_ff, a_fb, b_q, b_k, b_v, c_q, c_k, c_v, d_q, d_k, d_v, g_lat) = jnp.split(h @ w_in, SPLIT_AT, axis=-1)
    (a_qc, a_ic, a_ffc, a_fbc, b_qc, b_kc, b_vc, c_qc, c_kc, c_vc, d_qc, d_kc, d_vc, g_ctx) = jnp.split(hc @ w_in, SPLIT_AT, axis=-1)

    y_a, yc_a = hgrn2_mixer(a_q, a_i, a_ff, a_fb, a_qc, a_ic, a_ffc, a_fbc, lb, hg_norm_g, ctx_out)

    qB = apply_rope(rms_norm(split_heads(b_q, GA_HEADS), ga_qn), rope64)
    kB = apply_rope(rms_norm(split_heads(b_k, GA_KV), ga_kn), rope64)
    vB = split_heads(b_v, GA_KV)
    kBc = rms_norm(split_heads(b_kc, GA_KV), ga_kn)
    vBc = split_heads(b_vc, GA_KV)
    y_b = merge_heads(global_gqa(qB, jnp.concatenate([kB, kBc], 1), jnp.concatenate([vB, vBc], 1)))

    bsz, n = x.shape[0], x.shape[1]
    n_ctx = cx.shape[1]
    def diff_qk(t, length, rotate):
        t = t.reshape(bsz, length, DF_HEADS * 2, DF_QK)
        if rotate:
            t = apply_rope(t, rope32)
        return t.reshape(bsz, length, DF_HEADS, 2, DF_QK)
    lam_init = 0.8 - 0.6 * math.exp(-0.3 * layer_idx)
    lamf = df_lam.astype(jnp.float32)
    lam = jnp.exp(jnp.sum(lamf[0] * lamf[1])) - jnp.exp(jnp.sum(lamf[2] * lamf[3])) + lam_init
    qC, kC, vC = diff_qk(c_q, n, True), diff_qk(c_k, n, True), split_heads(c_v, DF_HEADS)
    kCc, vCc = diff_qk(c_kc, n_ctx, False), split_heads(c_vc, DF_HEADS)
    y_c = merge_heads(diff_attention(qC, jnp.concatenate([kC, kCc], 1), jnp.concatenate([vC, vCc], 1),
                                     lam, lam_init, df_subln_g))

    qD = apply_rope(split_heads(d_q, WN_HEADS), rope64)
    kD = apply_rope(split_heads(d_k, WN_KV), rope64)
    vD = split_heads(d_v, WN_KV)
    kDc, vDc = split_heads(d_kc, WN_KV), split_heads(d_vc, WN_KV)
    y_d = merge_heads(window_gqa_latent(qD, kD, vD, kDc, vDc, wn_sink))

    y = jnp.concatenate([y_a, y_b, y_c, y_d], axis=-1) * jax.nn.silu(g_lat)
    x_new = layer_norm(alpha * x + gate[:, None] * (y @ w_out), ln_g, ln_b)
    if not ctx_out:
        return x_new, None

    yc_b = merge_heads(global_gqa(rms_norm(split_heads(b_qc, GA_HEADS), ga_qn), kBc, vBc))
    yc_c = merge_heads(diff_attention(diff_qk(c_qc, n_ctx, False), kCc, vCc, lam, lam_init, df_subln_g))
    yc_d = merge_heads(global_gqa(split_heads(d_qc, WN_HEADS), kDc, vDc, wn_sink))
    yc = jnp.concatenate([yc_a, yc_b, yc_c, yc_d], axis=-1) * jax.nn.silu(g_ctx)
    cx_new = layer_norm(alpha * cx + gate_c * (yc @ w_out), ln_g, ln_b)
    return x_new, cx_new


def setup_inputs(seed: int = 0) -> dict:
    key = jax.random.key(seed)
    ks = jax.random.split(key, 17)
    f32 = jnp.float32
    nrm = lambda k, shape, s: jax.random.normal(k, shape, f32) * s
    beta = (8.0 * DEPTH) ** -0.25
    return {
        "x": nrm(ks[0], (BATCH, SEQ, D_MODEL), 1.0),
        "c": nrm(ks[1], (BATCH, D_MODEL), 1.0),
        "ctx": nrm(ks[2], (BATCH, CTX_LEN, D_MODEL), 1.0),
        "c_ctx": nrm(ks[3], (D_MODEL,), 1.0),
        "w_in": nrm(ks[4], (DEPTH, D_MODEL, IN_WIDTH), D_MODEL ** -0.5),
        "w_out": nrm(ks[5], (DEPTH, D_MIX, D_MODEL), (D_MIX ** -0.5) * beta),
        "w_ada": nrm(ks[6], (DEPTH, D_MODEL, 3 * D_MODEL), 0.5 * D_MODEL ** -0.5),
        "b_ada": nrm(ks[7], (DEPTH, 3 * D_MODEL), 0.02),
        "ln_g": 1.0 + nrm(ks[8], (DEPTH, D_MODEL), 0.02),
        "ln_b": nrm(ks[9], (DEPTH, D_MODEL), 0.02),
        "hg_lb_logits": nrm(ks[10], (DEPTH, 2, HG_HEADS * HG_DK), 0.1),
        "hg_norm_g": 1.0 + nrm(ks[11], (DEPTH, HG_DV), 0.02),
        "ga_q_norm_g": 1.0 + nrm(ks[12], (DEPTH, HEAD_DIM), 0.02),
        "ga_k_norm_g": 1.0 + nrm(ks[13], (DEPTH, HEAD_DIM), 0.02),
        "df_lambda": nrm(ks[14], (DEPTH, 4, DF_QK), 0.1),
        "df_subln_g": 1.0 + nrm(ks[15], (DEPTH, DF_V), 0.02),
        "wn_sink": nrm(ks[16], (DEPTH, WN_HEADS), 0.5),
    }


def reference(x, c, ctx, c_ctx, w_in, w_out, w_ada, b_ada, ln_g, ln_b, hg_lb_logits, hg_norm_g,
              ga_q_norm_g, ga_k_norm_g, df_lambda, df_subln_g, wn_sink):
    n = x.shape[1]
    rope64 = axial_rope(n, HEAD_DIM)
    rope32 = axial_rope(n, DF_QK)
    lb_sm = jax.nn.softmax(hg_lb_logits.astype(jnp.float32), axis=0)
    lower_bounds = jnp.cumsum(lb_sm, axis=0) - lb_sm[0]
    cx = ctx
    for l in range(DEPTH):
        x, cx = mixer_layer(x, cx, c, c_ctx, w_in[l], w_out[l], w_ada[l], b_ada[l], ln_g[l], ln_b[l],
                            lower_bounds[l], hg_norm_g[l], ga_q_norm_g[l], ga_k_norm_g[l], df_lambda[l],
                            df_subln_g[l], wn_sink[l], rope64, rope32, l, l < DEPTH - 1)
    return x
```

```python
import math
import numpy as np
import concourse.bass as bass
import concourse.mybir as mybir
from concourse.bass_utils import run_bass_kernel_spmd

F32 = mybir.dt.float32
BF16 = mybir.dt.bfloat16
ALU = mybir.AluOpType
AF = mybir.ActivationFunctionType
AX = mybir.AxisListType

ENGS = ("pe", "act", "dve", "pool", "sp")
SEM_LIMIT = 8000
import os as _os0
NSLOT = int(_os0.environ.get("K_NSLOT", "8"))
STRICT_SAME_ENGINE = True

D = 1024
KT = 8
T = 2304
TT = 18
NCH = 36
NFM = 26 * 128
LN_EPS = 1e-5
RMS_EPS = 1e-6
THETA = 10000.0


class V:
    def __init__(self, ap, keys):
        self.ap = ap
        self.keys = keys

    def re(self, fn):
        return V(fn(self.ap), self.keys)

    def __getitem__(self, idx):
        return V(self.ap[idx], self.keys)


class _Sub:
    def __init__(self, buf, i):
        self.buf = buf
        self.i = i

    def __getitem__(self, idx):
        return V(self.buf.ap[idx], self.buf.k(self.i))


class Buf:
    _n = 0

    def __init__(self, ap, nsub=1):
        self.ap = ap
        self.nsub = nsub
        Buf._n += 1
        self.id = Buf._n

    def __getitem__(self, idx):
        return V(self.ap[idx], self.k())

    def sub(self, i):
        return _Sub(self, i)

    def k(self, i=None):
        if i is None:
            return [(self.id, j) for j in range(self.nsub)]
        if isinstance(i, (list, tuple, range)):
            return [(self.id, j) for j in i]
        return [(self.id, i)]


def Dr(ap):
    return V(ap, [])


class Sched:
    def __init__(self, nc):
        self.nc = nc
        self.ops = {e: [] for e in ENGS}
        self.last_w = {}
        self.readers = {}
        self.seen = {e: {} for e in ENGS}
        self.slot_cnt = [0] * NSLOT
        self.slot_rr = 0
        self.n_dma = 0

    def _add(self, eng, fn, reads, writes, is_dma):
        idx = len(self.ops[eng])
        deps = set()
        for k in reads:
            w = self.last_w.get(k)
            if w is not None:
                deps.add(w)
        for k in writes:
            w = self.last_w.get(k)
            if w is not None:
                deps.add(w)
            for r in self.readers.get(k, {}).values():
                deps.add(r)
        waits = {}
        for tok in deps:
            if tok[0] == 'c':
                _, se, si = tok
                if se == eng and not is_dma:
                    if eng == 'pe':
                        continue
                    if not STRICT_SAME_ENGINE and eng != 'pool':
                        raw = False
                        for k in reads:
                            if self.last_w.get(k) == tok:
                                raw = True
                                break
                        if not raw:
                            continue
                key = ('c', se)
                if self.seen[eng].get(key, -1) >= si:
                    continue
                if waits.get(key, -1) < si:
                    waits[key] = si
            else:
                _, sl, cnt = tok
                key = ('d', sl)
                if self.seen[eng].get(key, -1) >= cnt:
                    continue
                if waits.get(key, -1) < cnt:
                    waits[key] = cnt
        if is_dma:
            sl = self.slot_rr
            self.slot_rr = (self.slot_rr + 1) % NSLOT
            prev = self.slot_cnt[sl]
            if prev > 0:
                key = ('d', sl)
                if self.seen[eng].get(key, -1) < prev and waits.get(key, -1) < prev:
                    waits[key] = prev
            self.slot_cnt[sl] = prev + 1
            tok = ('d', sl, prev + 1)
            self.n_dma += 1
        else:
            tok = ('c', eng, idx)
        for key, v in waits.items():
            self.seen[eng][key] = v
        self.ops[eng].append(dict(fn=fn, waits=waits, tok=tok, signal=False))
        for k in reads:
            self.readers.setdefault(k, {})[tok[:2]] = tok
        for k in writes:
            self.last_w[k] = tok
            self.readers[k] = {}
        return tok

    def op(self, eng, fn, reads=(), writes=()):
        return self._add(eng, fn, reads, writes, False)

    def dma(self, fn, reads=(), writes=(), eng="sp"):
        return self._add(eng, fn, reads, writes, True)

    def barrier(self):
        last = {}
        for e in ENGS:
            for i in range(len(self.ops[e]) - 1, -1, -1):
                if self.ops[e][i]['tok'][0] == 'c':
                    last[e] = i
                    break
        for e in ENGS:
            waits = {}
            for se, si in last.items():
                if se == e:
                    continue
                key = ('c', se)
                if self.seen[e].get(key, -1) < si:
                    waits[key] = si
            for sl in range(NSLOT):
                c = self.slot_cnt[sl]
                if c > 0 and self.seen[e].get(('d', sl), -1) < c:
                    waits[('d', sl)] = c
            for key, v in waits.items():
                self.seen[e][key] = v
            self.ops[e].append(dict(fn=None, waits=waits, tok=('c', e, len(self.ops[e])), signal=False))

    def emit(self):
        nc = self.nc
        for e in ENGS:
            for o in self.ops[e]:
                for key, v in o['waits'].items():
                    if key[0] == 'c':
                        self.ops[key[1]][v]['signal'] = True
        semval = {}
        nsig = {}
        for e in ENGS:
            c = 0
            for i, o in enumerate(self.ops[e]):
                if o['signal'] and o['tok'][0] == 'c':
                    ep, v = divmod(c, SEM_LIMIT)
                    semval[(e, i)] = (ep, v + 1)
                    c += 1
            nsig[e] = c
        sems = {}
        for e in ENGS:
            nep = max(1, (nsig[e] + SEM_LIMIT - 1) // SEM_LIMIT)
            sems[e] = [nc.alloc_semaphore(f"sem_{e}_{j}") for j in range(nep)]
        slot_sems = [nc.alloc_semaphore(f"sem_dma_{j}") for j in range(NSLOT)]
        self.stats = {e: len(self.ops[e]) for e in ENGS}
        self.stats['nsig'] = dict(nsig)
        self.stats['ndma'] = self.n_dma

        def run(e, h):
            for i, o in enumerate(self.ops[e]):
                for key, v in o['waits'].items():
                    if key[0] == 'c':
                        ep, val = semval[(key[1], v)]
                        h.wait_ge(sems[key[1]][ep], val)
                    else:
                        h.wait_ge(slot_sems[key[1]], 16 * v)
                if o['fn'] is None:
                    if o['signal']:
                        ep, val = semval[(e, i)]
                        h.nop().then_inc(sems[e][ep], 1)
                    continue
                inst = o['fn'](h)
                if o['tok'][0] == 'd':
                    inst.then_inc(slot_sems[o['tok'][1]], 16)
                elif o['signal']:
                    ep, val = semval[(e, i)]
                    inst.then_inc(sems[e][ep], 1)

        with nc.Block() as block:
            @block.tensor
            def _(h):
                run("pe", h)

            @block.scalar
            def _(h):
                run("act", h)

            @block.vector
            def _(h):
                run("dve", h)

            @block.gpsimd
            def _(h):
                run("pool", h)

            @block.sync
            def _(h):
                run("sp", h)


class Rot:
    def __init__(self, bufs):
        self.bufs = bufs
        self.i = 0

    def get(self):
        b = self.bufs[self.i % len(self.bufs)]
        self.i += 1
        return b


def _partner(d, dim):
    q = dim // 4
    return d + q if (d // q) in (0, 2) else d - q


def fm_columns():
    cols = []
    cols += list(range(0, 256))
    cols += list(range(512, 768))
    cols += list(range(768, 1024))

    def heads(base, order, perm):
        out = []
        for hh in order:
            for d in range(64):
                out.append(base + hh * 64 + (_partner(d, 64) if perm else d))
        return out

    def units32(base, n, perm):
        out = []
        for u in range(n):
            for d in range(32):
                out.append(base + u * 32 + (_partner(d, 32) if perm else d))
        return out

    for perm in (False, True):
        cols += heads(1024, [0, 2], perm) + heads(1024, [1, 3], perm) + heads(1280, [0, 1], perm)
    for perm in (False, True):
        cols += units32(1536, 8, perm) + units32(1792, 8, perm)
    for perm in (False, True):
        cols += heads(2304, [0, 2], perm) + heads(2304, [1, 3], perm) + heads(2560, [0, 1], perm)
    assert len(cols) == NFM
    return np.array(cols, dtype=np.int64)


def tm_columns():
    cols = list(range(256, 512)) + list(range(1408, 1536)) + list(range(2688, 2816)) + list(range(2048, 2304))
    return np.array(cols, dtype=np.int64)


def rope_tables():
    n_tok = 2048
    rows = n_tok // 64
    row = np.repeat(np.arange(rows, dtype=np.float32), 64)
    col = np.tile(np.arange(64, dtype=np.float32), rows)
    out = np.zeros((4, 128, n_tok), np.float32)
    for ti, dim in ((0, 64), (2, 32)):
        d_axis = dim // 2
        inv = (np.float32(THETA) ** (-np.arange(0, d_axis, 2, dtype=np.float32) / np.float32(d_axis))).astype(np.float32)
        ang_r = row[:, None] * inv
        ang_c = col[:, None] * inv
        ang = np.concatenate([ang_r, ang_r, ang_c, ang_c], axis=-1).astype(np.float32)
        cos = np.cos(ang).astype(np.float32).T
        sin = np.sin(ang).astype(np.float32).T
        q = dim // 4
        sign = np.array([-1.0 if (d // q) in (0, 2) else 1.0 for d in range(dim)], np.float32)[:, None]
        reps = 128 // dim
        out[ti] = np.tile(cos, (reps, 1))
        out[ti + 1] = np.tile(sin * sign, (reps, 1))
    return out


NCONST = 128 * 4 + 64 * 2 + 4
C_ID, C_BO, C_MP, C_MN, C_HF, C_HB, C_M12 = 0, 128, 256, 384, 512, 576, 640


def const_table():
    c = np.zeros((128, NCONST), np.float32)
    c[:, C_ID:C_ID + 128] = np.eye(128, dtype=np.float32)
    bo = np.zeros((128, 128), np.float32)
    bo[:64, :64] = 1.0
    bo[64:, 64:] = 1.0
    c[:, C_BO:C_BO + 128] = bo
    k = np.arange(128)[:, None]
    q = np.arange(128)[None, :]
    c[:, C_MP:C_MP + 128] = (k >= q)
    c[:, C_MN:C_MN + 128] = (k <= q)
    s = np.arange(64)[:, None]
    t = np.arange(64)[None, :]
    c[:64, C_HF:C_HF + 64] = (s <= t)
    c[:64, C_HB:C_HB + 64] = (s >= t)
    p = np.arange(128)
    c[:, C_M12] = ((p % 64) < 32)
    c[:, C_M12 + 1] = ((p % 64) >= 32)
    c[:, C_M12 + 2] = 1.0
    return c


def build(NS=2, NL=2, dbg=False, phases=("p0", "p1", "a", "b", "c", "d", "p3")):
    nc = bass.Bass("TRN2", target_bir_lowering=False)
    S = Sched(nc)
    NR = NS + 1

    def din(name, shape, dt=F32):
        return nc.dram_tensor(name, list(shape), dt, kind="ExternalInput").ap()

    def dscr(name, shape, dt):
        return nc.dram_tensor(name, list(shape), dt, kind=("ExternalOutput" if dbg else "Internal")).ap()

    x_in = din("x", [NS, 2048, D])
    ctx_in = din("ctx", [NS, 256, D])
    cT_in = din("cT", [128, KT, NR])
    wfm_in = din("wfm", [NL, D, NFM])
    wtm_in = din("wtm", [NL, D, 768])
    wg_in = din("wg", [NL, D, D])
    wout_in = din("wout", [NL, D, D])
    wada_in = din("wada", [NL, D, 3 * D])
    bada_in = din("bada", [NL, 3 * D])
    lng_in = din("lng", [NL, D])
    lnb_in = din("lnb", [NL, D])
    lbl_in = din("lbl", [128, 8])
    gqk_in = din("gqk", [NL, 128, 4])
    hgg_in = din("hgg", [NL, 64])
    dfl_in = din("dfl", [NL, 128])
    dfg_in = din("dfg", [NL, 64])
    snk_in = din("snk", [NL, 4])
    const_in = din("consts", [128, NCONST])
    rope_in = din("rope", [4, 128, 2048])
    out_d = nc.dram_tensor("out", [NS, 2048, D], F32, kind="ExternalOutput").ap()

    ADA = dscr("ADA", [NL, NR, 3 * D], F32)
    AQT = dscr("AQT", [NS, 256, T], BF16)
    AKT = dscr("AKT", [2, NS, 256, T], BF16)
    ALT = dscr("ALT", [2, NS, 256, T], F32)
    TMV = dscr("TMV", [NS, T, 768], BF16)
    QTB = dscr("QTB", [NS, 256, T], BF16)
    KTB = dscr("KTB", [NS, 128, T], BF16)
    QTC = dscr("QTC", [NS, 256, T], BF16)
    KTC = dscr("KTC", [2, NS, 256, T], BF16)
    QTD = dscr("QTD", [NS, 256, T], BF16)
    KTD = dscr("KTD", [NS, 128, T], BF16)
    OFD = dscr("OFD", [NS, T, 256], F32)
    YD = dscr("YD", [NS, T, D], F32)
    X1 = dscr("X1", [NS, T, D], F32)

    AR_WORDS = 48500
    arena_t = nc.alloc_sbuf_tensor("arena", [128, AR_WORDS], F32)
    cst_t = nc.alloc_sbuf_tensor("cst", [128, NCONST], F32)
    cstb_t = nc.alloc_sbuf_tensor("cstb", [128, NCONST], BF16)
    ones_t = nc.alloc_sbuf_tensor("onesb", [128, T], BF16)
    cst = Buf(cst_t[:])
    cstb = Buf(cstb_t[:])
    onesb = Buf(ones_t[:])
    ar = {"off": 0, "base": 0}

    def alloc(shape, dt, nsub=1):
        n = int(np.prod(shape[1:]))
        esz = 4 if dt == F32 else 2
        nw = (n * esz + 31) // 32 * 8
        off = ar["off"]
        assert off + nw <= AR_WORDS, f"arena overflow {off}+{nw}"
        ar["off"] = off + nw
        ap = arena_t[0:shape[0], off:off + nw]
        if dt == BF16:
            ap = ap.bitcast(BF16)
        ap = ap[:, 0:n]
        if len(shape) == 3:
            ap = ap.rearrange("p (a b) -> p a b", a=shape[1])
        elif len(shape) == 4:
            ap = ap.rearrange("p (a b c) -> p a b c", a=shape[1], b=shape[2])
        return Buf(ap, nsub)

    def rot(n, shape, dt, nsub=1):
        return Rot([alloc(shape, dt, nsub) for _ in range(n)])

    def arena_reset(keep=None):
        ar["off"] = ar["base"] if keep is None else keep

    psum_t = [nc.alloc_psum_tensor(f"psb{i}", [128, 512], F32) for i in range(8)]
    PS = [Buf(t[:]) for t in psum_t]

    def _rw(outs, ins):
        r, w = [], []
        for v in ins:
            r.extend(v.keys)
        for v in outs:
            w.extend(v.keys)
        return r, w

    def mm(out, lhsT, rhs, start=True, stop=True, skip=False):
        r, w = _rw([out], [lhsT, rhs])
        S.op("pe", lambda h: h.matmul(out.ap, lhsT=lhsT.ap, rhs=rhs.ap, start=start, stop=stop,
                                      skip_group_check=skip), r, w)

    def tr(out, in_, ident):
        r, w = _rw([out], [in_, ident])
        S.op("pe", lambda h: h.transpose(out=out.ap, in_=in_.ap, identity=ident.ap), r, w)

    def act(out, in_, func, scale=None, bias=None):
        ins = [in_]
        kw = {}
        if scale is not None:
            if isinstance(scale, V):
                ins.append(scale)
                kw["scale"] = scale.ap
            else:
                kw["scale"] = float(scale)
        if bias is not None:
            if isinstance(bias, V):
                ins.append(bias)
                kw["bias"] = bias.ap
            else:
                kw["bias"] = float(bias)
        r, w = _rw([out], ins)
        S.op("act", lambda h: h.activation(out=out.ap, in_=in_.ap, func=func, **kw), r, w)

    def tt(eng, out, in0, in1, op):
        r, w = _rw([out], [in0, in1])
        S.op(eng, lambda h: h.tensor_tensor(out=out.ap, in0=in0.ap, in1=in1.ap, op=op), r, w)

    def ts(eng, out, in0, s1, s2, op0, op1=None):
        ins = [in0]
        a1 = s1
        a2 = s2
        if isinstance(s1, V):
            ins.append(s1)
            a1 = s1.ap
        if isinstance(s2, V):
            ins.append(s2)
            a2 = s2.ap
        r, w = _rw([out], ins)
        if op1 is None:
            S.op(eng, lambda h: h.tensor_scalar(out=out.ap, in0=in0.ap, scalar1=a1, scalar2=None, op0=op0), r, w)
        else:
            S.op(eng, lambda h: h.tensor_scalar(out=out.ap, in0=in0.ap, scalar1=a1, scalar2=a2, op0=op0, op1=op1), r, w)

    def stt(out, in0, scalar, in1, op0, op1):
        ins = [in0, in1]
        a = scalar
        if isinstance(scalar, V):
            ins.append(scalar)
            a = scalar.ap
        r, w = _rw([out], ins)
        S.op("dve", lambda h: h.scalar_tensor_tensor(out=out.ap, in0=in0.ap, scalar=a, in1=in1.ap, op0=op0, op1=op1), r, w)

    def cp(eng, out, in_):
        r, w = _rw([out], [in_])
        if eng == "act":
            S.op("act", lambda h: h.activation(out=out.ap, in_=in_.ap, func=AF.Identity), r, w)
        else:
            S.op(eng, lambda h: h.tensor_copy(out=out.ap, in_=in_.ap), r, w)

    def memset(eng, out, val):
        r, w = _rw([out], [])
        S.op(eng, lambda h: h.memset(out.ap, val), r, w)

    def recip(out, in_):
        r, w = _rw([out], [in_])
        S.op("dve", lambda h: h.reciprocal(out=out.ap, in_=in_.ap), r, w)

    def dma(out, in_, slow=False):
        r, w = _rw([out], [in_])
        if slow:
            S.dma(lambda h: h.dma_start(out=out.ap, in_=in_.ap, allow_slow_non_contiguous=True), r, w)
        else:
            S.dma(lambda h: h.dma_start(out=out.ap, in_=in_.ap), r, w)

    def bc(v, shape):
        return v.re(lambda a: a.broadcast_to(list(shape)))

    dma(cst[:], Dr(const_in))
    cp("dve", cstb[:], cst[:])
    memset("pool", onesb[:], 1.0)
    identf = cst[:, C_ID:C_ID + 128]
    identb = cstb[:, C_ID:C_ID + 128]
    blockones = cstb[:, C_BO:C_BO + 128]
    maskP = cstb[:, C_MP:C_MP + 128]
    maskN = cstb[:, C_MN:C_MN + 128]
    hmaskF = cstb[0:64, C_HF:C_HF + 64]
    hmaskB = cstb[0:64, C_HB:C_HB + 64]
    m1 = cst[:, C_M12:C_M12 + 1]
    m2 = cst[:, C_M12 + 1:C_M12 + 2]

    cT = alloc([128, KT, NR], F32)
    scT = alloc([128, KT, NR], BF16)
    sc1T = alloc([128, KT, NR], F32)
    shT = alloc([128, KT, NR], F32)
    gateB = alloc([128, NR, D], F32)
    lngB = alloc([128, D], F32)
    lnbB = alloc([128, D], F32)
    lbl = alloc([128, 8], F32)
    lbp = alloc([128, 2, 4], F32)
    gqk = alloc([128, 4], F32)
    gA = alloc([64, 256], F32)
    dfl = alloc([128, 128], F32)
    lamt = alloc([128, 8], F32)
    gC = alloc([128, 64], F32)
    esink = alloc([128, 4], F32)
    hm4 = [alloc([64, 4, 64], BF16) for _ in range(2)]
    ar["base"] = ar["off"]
    for d_ in range(2):
        for hh in range(4):
            cp("pool", hm4[d_][:, hh, :], hmaskF if d_ == 0 else hmaskB)

    dma(cT[:], Dr(cT_in))
    act(scT[:], cT[:], AF.Silu)
    dma(lbl[:], Dr(lbl_in))

    def src_tile(l, s, c):
        if l == 0:
            if c < 2:
                return Dr(ctx_in[s, c * 128:(c + 1) * 128, :])
            return Dr(x_in[s, (c - 2) * 128:(c - 1) * 128, :])
        return Dr(X1[s, c * 128:(c + 1) * 128, :])

    def load_weights_bf16(dst, srcs, ncols):
        CH = 1024
        stg = rot(2, [128, CH], F32)
        engs = ["dve", "pool", "act"]
        i = 0
        for kt in range(KT):
            for c0 in range(0, ncols, CH):
                c1 = min(ncols, c0 + CH)
                st = stg.get()
                for (sap, off, n) in srcs:
                    lo, hi = max(c0, off), min(c1, off + n)
                    if lo < hi:
                        dma(st[:, lo - c0:hi - c0], Dr(sap[kt * 128:(kt + 1) * 128, lo - off:hi - off]))
                cp(engs[i % 3], dst[:, kt, c0:c1], st[:, 0:c1 - c0])
                i += 1

    def phase0(l):
        arena_reset()
        wadab = alloc([128, KT, 3 * D], BF16)
        load_weights_bf16(wadab, [(wada_in[l], 0, 3 * D)], 3 * D)
        badaB = alloc([NR, 3 * D], F32)
        dma(badaB[:], Dr(bada_in[l:l + 1, :].broadcast_to([NR, 3 * D])))
        adasb = alloc([NR, 3 * D], F32)
        for ng in range(6):
            ps = PS[ng % 4]
            for kt in range(KT):
                mm(ps[0:NR, 0:512], scT[:, kt, :], wadab[:, kt, ng * 512:(ng + 1) * 512], kt == 0, kt == KT - 1)
            tt("dve", adasb[:, ng * 512:(ng + 1) * 512], ps[0:NR, 0:512], badaB[:, ng * 512:(ng + 1) * 512], ALU.add)
        dma(Dr(ADA[l]), adasb[:])
        S.barrier()
        for r_ in range(NR):
            dma(shT[:, :, r_], Dr(ADA[l, r_, 0:D].rearrange("(kt p) -> p kt", p=128)), slow=True)
            dma(sc1T[:, :, r_], Dr(ADA[l, r_, D:2 * D].rearrange("(kt p) -> p kt", p=128)), slow=True)
        ts("dve", sc1T[:], sc1T[:], 1.0, None, ALU.add)
        for r_ in range(NR):
            dma(gateB[:, r_, :], Dr(ADA[l, r_:r_ + 1, 2 * D:3 * D].broadcast_to([128, D])))
        dma(lngB[:], Dr(lng_in[l:l + 1, :].broadcast_to([128, D])))
        dma(lnbB[:], Dr(lnb_in[l:l + 1, :].broadcast_to([128, D])))
        if l == 0:
            memset("dve", lbp[:, 0, :], 0.0)
            memset("dve", lbp[:, 1, :], 1.0)
        else:
            dlt = alloc([128, 4], F32)
            tt("dve", dlt[:], lbl[:, 4:8], lbl[:, 0:4], ALU.subtract)
            act(lbp[:, 0, :], dlt[:], AF.Sigmoid)
            ts("dve", lbp[:, 1, :], lbp[:, 0, :], -1.0, 1.0, ALU.mult, ALU.add)
        dma(gqk[:], Dr(gqk_in[l]))
        ts("dve", gqk[:], gqk[:], 8.0, None, ALU.mult)
        for hh in range(4):
            dma(gA[:, hh * 64:(hh + 1) * 64], Dr(hgg_in[l:l + 1, :].broadcast_to([64, 64])))
        ts("dve", gA[:], gA[:], 8.0, None, ALU.mult)
        lam_init = 0.8 - 0.6 * math.exp(-0.3 * l)
        dma(dfl[:], Dr(dfl_in[l:l + 1, :].broadcast_to([128, 128])))
        pr = alloc([128, 2, 32], F32)
        tt("dve", pr[:, 0, :], dfl[:, 0:32], dfl[:, 32:64], ALU.mult)
        tt("dve", pr[:, 1, :], dfl[:, 64:96], dfl[:, 96:128], ALU.mult)
        S.op("dve", lambda h: h.tensor_reduce(out=lamt.ap[:, 0:2], in_=pr.ap, axis=AX.X, op=ALU.add), pr.k(), lamt.k())
        act(lamt[:, 2:4], lamt[:, 0:2], AF.Exp)
        tt("dve", lamt[:, 4:5], lamt[:, 3:4], lamt[:, 2:3], ALU.subtract)
        ts("dve", lamt[:, 5:6], lamt[:, 4:5], -lam_init, None, ALU.add)
        dma(gC[:], Dr(dfg_in[l:l + 1, :].broadcast_to([128, 64])))
        ts("dve", gC[:], gC[:], 8.0 * (1.0 - lam_init), None, ALU.mult)
        dma(esink[:], Dr(snk_in[l:l + 1, :].broadcast_to([128, 4])))
        act(esink[:], esink[:], AF.Exp)
        S.barrier()

    neglam = lamt[:, 5:6]

    def phase1(l):
        arena_reset()
        w1 = alloc([128, KT, NFM + 768], BF16)
        load_weights_bf16(w1, [(wfm_in[l], 0, NFM), (wtm_in[l], NFM, 768)], NFM + 768)
        xpool = rot(1, [128, 4, D], F32)
        hpool = rot(2, [128, KT, 512], BF16, nsub=KT)
        tf = rot(10, [128, 512], F32)
        tb = rot(10, [128, 512], BF16)
        rpool = rot(2, [128, 4, 512], F32)
        tmst = rot(2, [128, 4, 768], BF16)
        pst = Rot([PS[0], PS[1]])
        psm = Rot([PS[i] for i in range(2, 8)])
        evac = Rot(["act", "dve"])

        blist = [(s, blk) for s in range(NS) for blk in range(5)]

        def load_blk(bi):
            s, blk = blist[bi]
            c0 = 0 if blk == 0 else 2 + 4 * (blk - 1)
            NB = 2 if blk == 0 else 4
            xt = xpool.get()
            for n in range(NB):
                dma(xt[:, n, :], src_tile(l, s, c0 + n))
            rp = None
            if blk > 0:
                lat0 = c0 * 128 - 256
                rp = rpool.get()
                for i in range(4):
                    dma(rp[:, i, :], Dr(rope_in[i, :, lat0:lat0 + 512]))
            return xt, rp

        nxt = load_blk(0)
        for bi, (s, blk) in enumerate(blist):
            if True:
                c0 = 0 if blk == 0 else 2 + 4 * (blk - 1)
                NB = 2 if blk == 0 else 4
                W = 128 * NB
                t0 = c0 * 128
                row = NS if blk == 0 else s
                xt, rp = nxt
                hT = hpool.get()
                for kt in range(KT):
                    p = pst.get()
                    for n in range(NB):
                        tr(p[:, n * 128:(n + 1) * 128], xt[:, n, kt * 128:(kt + 1) * 128], identf)
                    o = hT.sub(kt)[:, kt, 0:W]
                    if kt % 2 == 0:
                        ts("dve", o, p[:, 0:W], sc1T[:, kt, row:row + 1], shT[:, kt, row:row + 1], ALU.mult, ALU.add)
                    else:
                        act(o, p[:, 0:W], AF.Identity, scale=sc1T[:, kt, row:row + 1], bias=shT[:, kt, row:row + 1])
                if bi + 1 < len(blist):
                    nxt = load_blk(bi + 1)

                def group(j):
                    ps = psm.get()
                    for kt in range(KT):
                        mm(ps[:, 0:W], w1[:, kt, j * 128:(j + 1) * 128], hT.sub(kt)[:, kt, 0:W], kt == 0, kt == KT - 1)
                    return ps

                def rows(dr3, j):
                    return Dr(dr3[j * 128:(j + 1) * 128, t0:t0 + W])

                for j in range(2):
                    ps = group(j)
                    o = tb.get()
                    act(o[:, 0:W], ps[:, 0:W], AF.Silu)
                    dma(rows(AQT[s], j), o[:, 0:W])
                for d_ in range(2):
                    for j in range(2):
                        ps = group(2 + 2 * d_ + j)
                        sg = tf.get()
                        act(sg[:, 0:W], ps[:, 0:W], AF.Sigmoid)
                        f = tf.get()
                        ts("dve", f[:, 0:W], sg[:, 0:W], lbp[:, 1, d_ * 2 + j:d_ * 2 + j + 1],
                           lbp[:, 0, d_ * 2 + j:d_ * 2 + j + 1], ALU.mult, ALU.add)
                        lg = tf.get()
                        act(lg[:, 0:W], f[:, 0:W], AF.Ln)
                        dma(rows(ALT[d_, s], j), lg[:, 0:W])
                        kk = tb.get()
                        ts("pool", kk[:, 0:W], f[:, 0:W], -1.0, 1.0, ALU.mult, ALU.add)
                        dma(rows(AKT[d_, s], j), kk[:, 0:W])
                for i in range(3):
                    ps = group(6 + i)
                    sq = tb.get()
                    act(sq[:, 0:W], ps[:, 0:W], AF.Square)
                    qraw = tf.get()
                    cp("dve", qraw[:, 0:W], ps[:, 0:W])
                    gcol = 0 if i < 2 else 2
                    psr = psm.get()
                    mm(psr[:, 0:W], blockones, sq[:, 0:W])
                    r_ = tf.get()
                    act(r_[:, 0:W], psr[:, 0:W], AF.Ln, bias=64.0 * RMS_EPS)
                    act(r_[:, 0:W], r_[:, 0:W], AF.Exp, scale=-0.5)
                    qn = tf.get()
                    stt(qn[:, 0:W], qraw[:, 0:W], gqk[:, gcol:gcol + 1], r_[:, 0:W], ALU.mult, ALU.mult)
                    o = tb.get()
                    if blk == 0:
                        cp("pool", o[:, 0:W], qn[:, 0:W])
                    else:
                        ps2 = group(9 + i)
                        qn2 = tf.get()
                        stt(qn2[:, 0:W], ps2[:, 0:W], gqk[:, gcol + 1:gcol + 2], r_[:, 0:W], ALU.mult, ALU.mult)
                        t1 = tf.get()
                        tt("pool", t1[:, 0:W], qn[:, 0:W], rp[:, 0, :], ALU.mult)
                        t2 = tf.get()
                        tt("pool", t2[:, 0:W], qn2[:, 0:W], rp[:, 1, :], ALU.mult)
                        tt("dve", o[:, 0:W], t1[:, 0:W], t2[:, 0:W], ALU.add)
                    if i < 2:
                        dma(rows(QTB[s], i), o[:, 0:W])
                    else:
                        dma(rows(KTB[s], 0), o[:, 0:W])
                for i in range(4):
                    ps = group(12 + i)
                    if blk == 0:
                        a = ps
                    else:
                        ps2 = group(16 + i)
                        t1 = tf.get()
                        tt("dve", t1[:, 0:W], ps[:, 0:W], rp[:, 2, :], ALU.mult)
                        t2 = tf.get()
                        tt("dve", t2[:, 0:W], ps2[:, 0:W], rp[:, 3, :], ALU.mult)
                        a = tf.get()
                        tt("pool", a[:, 0:W], t1[:, 0:W], t2[:, 0:W], ALU.add)
                    if i < 2:
                        o = tb.get()
                        cp("act", o[:, 0:W], a[:, 0:W])
                        dma(rows(QTC[s], i), o[:, 0:W])
                    else:
                        o1 = tb.get()
                        act(o1[:, 0:W], a[:, 0:W], AF.Identity, scale=m1)
                        dma(rows(KTC[0, s], i - 2), o1[:, 0:W])
                        o2 = tb.get()
                        act(o2[:, 0:W], a[:, 0:W], AF.Identity, scale=m2)
                        dma(rows(KTC[1, s], i - 2), o2[:, 0:W])
                for i in range(3):
                    ps = group(20 + i)
                    o = tb.get()
                    if blk == 0:
                        cp("act", o[:, 0:W], ps[:, 0:W])
                    else:
                        ps2 = group(23 + i)
                        t1 = tf.get()
                        tt("dve", t1[:, 0:W], ps[:, 0:W], rp[:, 0, :], ALU.mult)
                        t2 = tf.get()
                        tt("dve", t2[:, 0:W], ps2[:, 0:W], rp[:, 1, :], ALU.mult)
                        tt("pool", o[:, 0:W], t1[:, 0:W], t2[:, 0:W], ALU.add)
                    if i < 2:
                        dma(rows(QTD[s], i), o[:, 0:W])
                    else:
                        dma(rows(KTD[s], 0), o[:, 0:W])
                st = tmst.get()
                for n in range(NB):
                    for (c_lo, c_n) in ((0, 512), (512, 256)):
                        ps = psm.get()
                        for kt in range(KT):
                            mm(ps[:, 0:c_n], hT.sub(kt)[:, kt, n * 128:(n + 1) * 128],
                               w1[:, kt, NFM + c_lo:NFM + c_lo + c_n], kt == 0, kt == KT - 1)
                        cp(evac.get(), st[:, n, c_lo:c_lo + c_n], ps[:, 0:c_n])
                dma(Dr(TMV[s, t0:t0 + W, :].rearrange("(n p) d -> p n d", p=128)), st[:, 0:NB, :])
        S.barrier()

    def phaseA(l):
        for s in range(NS):
            arena_reset()
            qT = alloc([128, 2, T], BF16)
            vch = alloc([64, NCH, 256], BF16)
            dma(qT[:], Dr(AQT[s].rearrange("(j p) t -> p j t", p=128)))
            dma(vch[:], Dr(TMV[s, :, 0:256].rearrange("(c p) d -> p c d", p=64)))
            keep = ar["off"]
            for d_ in range(2):
                arena_reset(keep)
                kT = alloc([128, 2, T], BF16)
                lfT = alloc([128, 2, T], F32)
                E = alloc([128, 2, T + 1], F32)
                dma(kT[:], Dr(AKT[d_, s].rearrange("(j p) t -> p j t", p=128)))
                dma(lfT[:], Dr(ALT[d_, s].rearrange("(j p) t -> p j t", p=128)))
                memset("dve", E[:, :, 0:1], 0.0)
                for j in range(2):
                    r, w = _rw([E[:]], [lfT[:], onesb[:]])
                    S.op("dve", lambda h, j=j: h.tensor_tensor_scan(
                        out=E.ap[:, j, 1:T + 1], data0=onesb.ap[:, 0:T], data1=lfT.ap[:, j, :],
                        initial=0.0, op0=ALU.mult, op1=ALU.add), r, w)
                qe = alloc([128, 2, T], BF16)
                ketmp = alloc([128, T], BF16)
                kdT = alloc([128, 2, T], BF16)
                keA = alloc([128, 2, T], BF16)
                keB = alloc([128, 2, T], BF16)
                qeD = alloc([128, 2, NCH, 128], BF16)
                qeDB = alloc([128, 2, NCH, 128], BF16)
                memset("pool", keA[:], 0.0)
                memset("pool", keB[:], 0.0)
                memset("pool", qeD[:], 0.0)
                memset("pool", qeDB[:], 0.0)
                hv = lambda v: v.re(lambda a: a.rearrange("p (c t) -> p c t", t=64))
                lo_, hi_ = slice(0, 32), slice(32, 64)
                sa, sb_ = (lo_, hi_) if d_ == 0 else (hi_, lo_)
                emid = alloc([128, 2, NCH], F32)
                etot = alloc([128, 2, NCH], F32)
                tmpA = lfT.sub(0)[:, 0, :]
                tmpB = lfT.sub(0)[:, 1, :]

                def cv(v):
                    return v.re(lambda a: a.rearrange("p (c t) -> p c t", t=64))

                for j in range(2):
                    Elo = cv(E[:, j, 0:T])
                    Ehi = cv(E[:, j, 1:T + 1])
                    Eend = Ehi[:, :, 63:64]
                    Emid = Elo[:, :, 32:33]
                    Est = Elo[:, :, 0:1]
                    shp = [128, NCH, 64]
                    if d_ == 0:
                        tt("dve", cv(tmpA[:]), Ehi, bc(Emid, shp), ALU.subtract)
                        act(tmpB[:], tmpA[:], AF.Exp)
                        tt("dve", qe[:, j, :], qT[:, j, :], tmpB[:], ALU.mult)
                        act(tmpB[:], tmpA[:], AF.Exp, scale=-1.0)
                        tt("dve", ketmp[:], kT[:, j, :], tmpB[:], ALU.mult)
                        cp("pool", hv(keA[:, j, :])[:, :, sa], hv(ketmp[:])[:, :, sa])
                        cp("pool", hv(keB[:, j, :])[:, :, sb_], hv(ketmp[:])[:, :, sb_])
                        tt("dve", cv(tmpA[:]), bc(Eend, shp), Ehi, ALU.subtract)
                        act(tmpB[:], tmpA[:], AF.Exp)
                        tt("dve", kdT[:, j, :], kT[:, j, :], tmpB[:], ALU.mult)
                        tt("dve", tmpA[:, 0:NCH], Emid.re(lambda a: a[:, :, 0]), Est.re(lambda a: a[:, :, 0]), ALU.subtract)
                    else:
                        tt("dve", cv(tmpA[:]), Elo, bc(Emid, shp), ALU.subtract)
                        act(tmpB[:], tmpA[:], AF.Exp, scale=-1.0)
                        tt("dve", qe[:, j, :], qT[:, j, :], tmpB[:], ALU.mult)
                        act(tmpB[:], tmpA[:], AF.Exp)
                        tt("dve", ketmp[:], kT[:, j, :], tmpB[:], ALU.mult)
                        cp("pool", hv(keA[:, j, :])[:, :, sa], hv(ketmp[:])[:, :, sa])
                        cp("pool", hv(keB[:, j, :])[:, :, sb_], hv(ketmp[:])[:, :, sb_])
                        tt("dve", cv(tmpA[:]), Elo, bc(Est, shp), ALU.subtract)
                        act(tmpB[:], tmpA[:], AF.Exp)
                        tt("dve", kdT[:, j, :], kT[:, j, :], tmpB[:], ALU.mult)
                        tt("dve", tmpA[:, 0:NCH], Eend.re(lambda a: a[:, :, 0]), Emid.re(lambda a: a[:, :, 0]), ALU.subtract)
                    act(emid[:, j, :], tmpA[:, 0:NCH], AF.Exp)
                    tt("dve", tmpA[:, 64:64 + NCH], Eend.re(lambda a: a[:, :, 0]), Est.re(lambda a: a[:, :, 0]), ALU.subtract)
                    act(etot[:, j, :], tmpA[:, 64:64 + NCH], AF.Exp)

                for j in range(2):
                    for pb in (0, 64):
                        cp("pool", qeD[pb:pb + 64, j, :, pb:pb + 64], hv(qe[pb:pb + 64, j, :]))
                        off = pb + (32 if sb_ is hi_ else 0)
                        cp("pool", qeDB[pb:pb + 64, j, :, off:off + 32], hv(qe[pb:pb + 64, j, :])[:, :, sb_])
                import os as _os
                KA = int(_os.environ.get("KA_STOP", "99"))
                if KA <= 2:
                    S.barrier()
                    continue
                Sst = alloc([128, 2, 64], F32)
                Sb = alloc([128, 2, 128], BF16)
                memset("dve", Sst[:], 0.0)
                memset("dve", Sb[:], 0.0)
                attp = rot(2, [64, 4, 64], BF16)
                kdp = rot(2, [64, 256], BF16)
                osb = rot(3, [64, 256], F32)
                ofp = rot(3, [64, 256], F32)
                sqp = rot(2, [64, 256], F32)
                smp = rot(2, [64, 8], F32)
                yap = rot(3, [64, 256], F32)
                hmask = hmaskF if d_ == 0 else hmaskB
                order = list(range(NCH)) if d_ == 0 else [3, 2, 1, 0] + list(range(NCH - 1, 3, -1))
                ps_att = Rot([PS[0], PS[1]])
                ps_kd = Rot([PS[2], PS[6]])
                ps_o = Rot([(PS[3], PS[4])])
                ps_s = Rot([PS[5], PS[7]])
                ev = lambda v, par: v.re(lambda a: a.rearrange("p (a b d) -> p a b d", a=2, b=2)[:, :, par, :])
                pv2 = lambda p_: p_[0:64, 0:128].re(lambda a: a.rearrange("p (a d) -> p a d", a=2))
                for ci, c in enumerate(order):
                    if ci >= int(_os.environ.get("KA_NCH", "99")):
                        break
                    cs = slice(c * 64, (c + 1) * 64)
                    need_o = not (l == NL - 1 and c < 4)
                    KO = int(_os.environ.get("KA_O", "15"))
                    if not (KO & 1):
                        need_o = False
                    if need_o:
                        pa = ps_att.get()
                        for j in range(2):
                            o_ = pa[0:64, j * 128:(j + 1) * 128]
                            mm(o_, keA[:, j, cs], qeD[:, j, c, :], True, False)
                            mm(o_, keB[:, j, cs], qeDB[:, j, c, :], False, True)
                        am = attp.get()
                        tt("dve", am[:].re(lambda a: a.rearrange("p h t -> p (h t)")), pa[0:64, 0:256],
                           hm4[d_][:].re(lambda a: a.rearrange("p h t -> p (h t)")), ALU.mult)
                    pk = ps_kd.get()
                    pkb = pk[:].re(lambda a: a.bitcast(BF16))
                    for j in range(2):
                        tr(pkb[0:64, j * 128:(j + 1) * 128], kdT[:, j, cs], identb)
                    kd = kdp.get()
                    cp("act", kd[:], pkb[0:64, 0:256])
                    if need_o and (KO & 2):
                        po = ps_o.get()
                        for j in range(2):
                            mm(po[0][0:64, j * 128:(j + 1) * 128], qe[:, j, cs], Sb[:, j, :], j == 0, False, skip=True)
                        for hh in range(4):
                            mm(po[0][0:64, hh * 64:(hh + 1) * 64], am[:, hh, :], vch[:, c, hh * 64:(hh + 1) * 64],
                               False, hh == 3, skip=True)
                        if not (KO & 4):
                            pass
                        elif d_ == 0:
                            ob = osb.get()
                            cp("act", ob[:], po[0][0:64, 0:256])
                            dma(Dr(OFD[s, cs, :]), ob[:])
                        else:
                            of = ofp.get()
                            dma(of[:], Dr(OFD[s, cs, :]))
                            osum = osb.get()
                            tt("dve", osum[:], po[0][0:64, 0:256], of[:], ALU.add)
                            sq = sqp.get()
                            act(sq[:], osum[:], AF.Square)
                            sm = smp.get()
                            S.op("dve", lambda h, sm=sm, sq=sq: h.tensor_reduce(
                                out=sm.ap[:, 0:4], in_=sq.ap.rearrange("p (h d) -> p h d", h=4), axis=AX.X, op=ALU.add),
                                sq.k(), sm.k())
                            act(sm[:, 4:8], sm[:, 0:4], AF.Ln, bias=64.0 * RMS_EPS)
                            act(sm[:, 4:8], sm[:, 4:8], AF.Exp, scale=-0.5)
                            ya = yap.get()
                            tt("dve", ya[:].re(lambda a: a.rearrange("p (h d) -> p h d", h=4)),
                               osum[:].re(lambda a: a.rearrange("p (h d) -> p h d", h=4)),
                               sm[:, 4:8].re(lambda a: a.unsqueeze(2).broadcast_to([64, 4, 64])), ALU.mult)
                            tt("pool", ya[:], ya[:], gA[:], ALU.mult)
                            dma(Dr(YD[s, cs, 0:256]), ya[:])
                    if ci == len(order) - 1:
                        break
                    KM = int(_os.environ.get("KA_MODE", "15"))
                    pss = ps_s.get()
                    if KM & 2:
                      for j in range(2):
                        mm(pss[:, j * 128:(j + 1) * 128], kd[:, j * 128:(j + 1) * 128], vch[:, c, j * 128:(j + 1) * 128])
                    if KM & 4:
                      for hh in range(4):
                        j, pb = hh // 2, (hh % 2) * 64
                        stt(Sst[pb:pb + 64, j, :], Sst[pb:pb + 64, j, :], etot[pb:pb + 64, j, c:c + 1],
                            pss[pb:pb + 64, j * 128 + (hh % 2) * 64:j * 128 + (hh % 2) * 64 + 64], ALU.mult, ALU.add)
                    c2 = order[ci + 1]
                    if KM & 8:
                      for j in range(2):
                        for pb in (0, 64):
                            ts("dve", Sb[pb:pb + 64, j, pb:pb + 64], Sst[pb:pb + 64, j, :],
                               emid[pb:pb + 64, j, c2:c2 + 1], None, ALU.mult)
                S.barrier()

    def load_v(dst, s, col0, nh):
        memset("pool", dst[:], 1.0)
        for hh in range(nh):
            dma(dst[:, :, hh, 0:64],
                Dr(TMV[s, :, col0 + hh * 64:col0 + (hh + 1) * 64].rearrange("(c p) d -> p c d", p=128)))

    def qblocks(l):
        blks = []
        if l < NL - 1:
            blks.append((0, 2))
        for b_ in range(4):
            blks.append((2 + 4 * b_, 4))
        return blks

    def phaseB(l):
        for s in range(NS):
            arena_reset()
            ktb = alloc([128, T], BF16)
            vb = alloc([128, TT, 2, 65], BF16)
            dma(ktb[:], Dr(KTB[s]))
            load_v(vb, s, 256, 2)
            qtp = rot(2, [128, 2, 512], BF16)
            ptp = rot(4, [128, 512], BF16)
            ysp = rot(2, [128, 4, 256], F32)
            rsp = rot(2, [128, 4], F32)
            pss = Rot([PS[i] for i in range(0, 5)])
            pop = Rot([PS[5], PS[6], PS[7]])
            for (c0, NB) in qblocks(l):
                W = NB * 128
                t0 = c0 * 128
                qt = qtp.get()
                dma(qt[:, :, 0:W], Dr(QTB[s, :, t0:t0 + W].rearrange("(j p) t -> p j t", p=128)))
                ys = ysp.get()
                keys = [0, 1] if c0 == 0 else list(range(TT))
                for hh in range(4):
                    j, pb, kv = hh % 2, (hh // 2) * 64, hh // 2
                    po = pop.get()
                    for ki, kc in enumerate(keys):
                        ps = pss.get()
                        mm(ps[:, 0:W], ktb[pb:pb + 64, kc * 128:(kc + 1) * 128], qt[pb:pb + 64, j, 0:W])
                        pt = ptp.get()
                        act(pt[:, 0:W], ps[:, 0:W], AF.Exp, scale=0.125)
                        for n in range(NB):
                            mm(po[:, n * 65:(n + 1) * 65], pt[:, n * 128:(n + 1) * 128], vb[:, kc, kv, :],
                               ki == 0 and n == 0, ki == len(keys) - 1, skip=True)
                    pov = po[:, 0:NB * 65].re(lambda a: a.rearrange("p (n d) -> p n d", d=65))
                    rs = rsp.get()
                    recip(rs[:, 0:NB], pov[:, :, 64])
                    tt("dve", ys[:, 0:NB, hh * 64:(hh + 1) * 64], pov[:, :, 0:64],
                       rs[:, 0:NB].re(lambda a: a.unsqueeze(2).broadcast_to([128, NB, 64])), ALU.mult)
                dma(Dr(YD[s, t0:t0 + W, 256:512].rearrange("(n p) d -> p n d", p=128)), ys[:, 0:NB, :])
            S.barrier()

    def phaseC(l):
        sc_c = 32.0 ** -0.5
        for s in range(NS):
            arena_reset()
            ktc = [alloc([128, 2, T], BF16) for _ in range(2)]
            vc = alloc([128, TT, 4, 65], BF16)
            for m in range(2):
                dma(ktc[m][:], Dr(KTC[m, s].rearrange("(j p) t -> p j t", p=128)))
            load_v(vc, s, 512, 4)
            qtp = rot(2, [128, 2, 512], BF16)
            ptp = rot(4, [128, 512], BF16)
            ysp = rot(2, [128, 4, 256], F32)
            rsp = rot(2, [128, 16], F32)
            o1p = rot(2, [128, 4, 64], F32)
            o2p = rot(2, [128, 4, 64], F32)
            pss = Rot([PS[i] for i in range(0, 4)])
            pop = Rot([PS[4], PS[5], PS[6], PS[7]])
            for (c0, NB) in qblocks(l):
                W = NB * 128
                t0 = c0 * 128
                qt = qtp.get()
                dma(qt[:, :, 0:W], Dr(QTC[s, :, t0:t0 + W].rearrange("(j p) t -> p j t", p=128)))
                ys = ysp.get()
                keys = [0, 1] if c0 == 0 else list(range(TT))
                for hh in range(4):
                    j, pb = hh // 2, (hh % 2) * 64
                    pos = [pop.get(), pop.get()]
                    for ki, kc in enumerate(keys):
                        for m in range(2):
                            ps = pss.get()
                            mm(ps[:, 0:W], ktc[m][pb:pb + 64, j, kc * 128:(kc + 1) * 128], qt[pb:pb + 64, j, 0:W])
                            pt = ptp.get()
                            act(pt[:, 0:W], ps[:, 0:W], AF.Exp, scale=sc_c)
                            for n in range(NB):
                                mm(pos[m][:, n * 65:(n + 1) * 65], pt[:, n * 128:(n + 1) * 128], vc[:, kc, hh, :],
                                   ki == 0 and n == 0, ki == len(keys) - 1, skip=True)
                    pv = [p_[:, 0:NB * 65].re(lambda a: a.rearrange("p (n d) -> p n d", d=65)) for p_ in pos]
                    rs = rsp.get()
                    recip(rs[:, 0:NB], pv[0][:, :, 64])
                    recip(rs[:, 4:4 + NB], pv[1][:, :, 64])
                    ts("dve", rs[:, 8:8 + NB], rs[:, 4:4 + NB], neglam, None, ALU.mult)
                    o1 = o1p.get()
                    o2 = o2p.get()
                    b3 = [128, NB, 64]
                    tt("dve", o1[:, 0:NB, :], pv[0][:, :, 0:64], rs[:, 0:NB].re(lambda a: a.unsqueeze(2).broadcast_to(b3)), ALU.mult)
                    tt("dve", o2[:, 0:NB, :], pv[1][:, :, 0:64], rs[:, 8:8 + NB].re(lambda a: a.unsqueeze(2).broadcast_to(b3)), ALU.mult)
                    tt("pool", o1[:, 0:NB, :], o1[:, 0:NB, :], o2[:, 0:NB, :], ALU.add)
                    act(o2[:, 0:NB, :], o1[:, 0:NB, :], AF.Square)
                    S.op("dve", lambda h, rs=rs, o2=o2, NB=NB: h.tensor_reduce(
                        out=rs.ap[:, 12:12 + NB], in_=o2.ap[:, 0:NB, :], axis=AX.X, op=ALU.add), o2.k(), rs.k())
                    act(rs[:, 12:12 + NB], rs[:, 12:12 + NB], AF.Ln, bias=64.0 * RMS_EPS)
                    act(rs[:, 12:12 + NB], rs[:, 12:12 + NB], AF.Exp, scale=-0.5)
                    tt("dve", o1[:, 0:NB, :], o1[:, 0:NB, :], rs[:, 12:12 + NB].re(lambda a: a.unsqueeze(2).broadcast_to(b3)), ALU.mult)
                    tt("pool", ys[:, 0:NB, hh * 64:(hh + 1) * 64], o1[:, 0:NB, :],
                       gC[:].re(lambda a: a.unsqueeze(1).broadcast_to(b3)), ALU.mult)
                dma(Dr(YD[s, t0:t0 + W, 512:768].rearrange("(n p) d -> p n d", p=128)), ys[:, 0:NB, :])
            S.barrier()

    def phaseD(l):
        for s in range(NS):
            arena_reset()
            ktd = alloc([128, T], BF16)
            qtd = alloc([128, 2, T], BF16)
            vd = alloc([128, TT, 2, 65], BF16)
            dma(ktd[:], Dr(KTD[s]))
            dma(qtd[:], Dr(QTD[s].rearrange("(j p) t -> p j t", p=128)))
            load_v(vd, s, 384, 2)
            ptp = rot(3, [128, 640], BF16)
            ysp = rot(3, [128, 256], F32)
            rsp = rot(3, [128, 2], F32)
            psA = Rot([PS[0], PS[2], PS[4]])
            psB = Rot([PS[1], PS[3], PS[5]])
            pop = Rot([PS[6], PS[7]])
            tiles = list(range(TT)) if l < NL - 1 else list(range(2, TT))
            for c in tiles:
                if c < 2:
                    klist = [(0, None), (1, None)]
                else:
                    klist = [(0, None), (1, None)]
                    if c - 1 >= 2:
                        klist.append((c - 1, maskP))
                    klist.append((c, None))
                    if c + 1 < TT:
                        klist.append((c + 1, maskN))
                ys = ysp.get()
                for hh in range(4):
                    j, pb, kv = hh % 2, (hh // 2) * 64, hh // 2
                    pa = psA.get()
                    pb_ = psB.get()
                    for i, (kc, mk) in enumerate(klist):
                        dst = pa[:, i * 128:(i + 1) * 128] if i < 4 else pb_[:, 0:128]
                        mm(dst, ktd[pb:pb + 64, kc * 128:(kc + 1) * 128], qtd[pb:pb + 64, j, c * 128:(c + 1) * 128])
                    pt = ptp.get()
                    n1 = min(len(klist), 4)
                    act(pt[:, 0:n1 * 128], pa[:, 0:n1 * 128], AF.Exp, scale=0.125)
                    if len(klist) > 4:
                        act(pt[:, 512:640], pb_[:, 0:128], AF.Exp, scale=0.125)
                    for i, (kc, mk) in enumerate(klist):
                        if mk is not None:
                            tt("pool", pt[:, i * 128:(i + 1) * 128], pt[:, i * 128:(i + 1) * 128], mk, ALU.mult)
                    po = pop.get()
                    for i, (kc, mk) in enumerate(klist):
                        mm(po[:, 0:65], pt[:, i * 128:(i + 1) * 128], vd[:, kc, kv, :], i == 0, i == len(klist) - 1)
                    rs = rsp.get()
                    ts("dve", rs[:, 0:1], po[:, 64:65], esink[:, hh:hh + 1], None, ALU.add)
                    recip(rs[:, 1:2], rs[:, 0:1])
                    ts("dve", ys[:, hh * 64:(hh + 1) * 64], po[:, 0:64], rs[:, 1:2], None, ALU.mult)
                dma(Dr(YD[s, c * 128:(c + 1) * 128, 768:1024]), ys[:])
            S.barrier()

    def phase3(l):
        arena_reset()
        alpha = (2.0 * 2) ** 0.25
        w3 = alloc([128, KT, 2 * D], BF16)
        load_weights_bf16(w3, [(wg_in[l], 0, D), (wout_in[l], D, D)], 2 * D)
        xp = rot(2, [128, D], F32)
        yp = rot(2, [128, D], F32)
        hp = rot(2, [128, KT, 128], BF16)
        sgp = rot(2, [128, D], F32)
        ygp = rot(2, [128, D], BF16)
        ygTp = rot(2, [128, KT, 128], BF16)
        tp = rot(2, [128, D], F32)
        zp = rot(2, [128, D], F32)
        stp = rot(2, [128, 16], F32)
        op_ = rot(2, [128, D], F32)
        pst = Rot([PS[0], PS[1]])
        psg = Rot([PS[2], PS[3]])
        psy = Rot([PS[4]])
        pso = Rot([PS[5], PS[6], PS[7]])
        tiles = list(range(TT)) if l < NL - 1 else list(range(2, TT))
        for s in range(NS):
            for c in tiles:
                row = NS if c < 2 else s
                xt = xp.get()
                yt = yp.get()
                dma(xt[:], src_tile(l, s, c))
                dma(yt[:], Dr(YD[s, c * 128:(c + 1) * 128, :]))
                hT = hp.get()
                for half in range(2):
                    p = pst.get()
                    for q in range(4):
                        kt = half * 4 + q
                        tr(p[:, q * 128:(q + 1) * 128], xt[:, kt * 128:(kt + 1) * 128], identf)
                    for q in range(4):
                        kt = half * 4 + q
                        if q % 2 == 0:
                            ts("dve", hT[:, kt, :], p[:, q * 128:(q + 1) * 128], sc1T[:, kt, row:row + 1],
                               shT[:, kt, row:row + 1], ALU.mult, ALU.add)
                        else:
                            act(hT[:, kt, :], p[:, q * 128:(q + 1) * 128], AF.Identity,
                                scale=sc1T[:, kt, row:row + 1], bias=shT[:, kt, row:row + 1])
                sg = sgp.get()
                for g in range(2):
                    ps = psg.get()
                    for kt in range(KT):
                        mm(ps[:, :], hT[:, kt, :], w3[:, kt, g * 512:(g + 1) * 512], kt == 0, kt == KT - 1)
                    act(sg[:, g * 512:(g + 1) * 512], ps[:, :], AF.Silu)
                yg = ygp.get()
                tt("pool", yg[:], yt[:], sg[:], ALU.mult)
                py = psy.get()
                pyb = py[:].re(lambda a: a.bitcast(BF16))
                for kt in range(KT):
                    tr(pyb[:, kt * 128:(kt + 1) * 128], yg[:, kt * 128:(kt + 1) * 128], identb)
                ygT = ygTp.get()
                cp("act", ygT[:].re(lambda a: a.rearrange("p a b -> p (a b)")), pyb[:, 0:1024])
                t_ = tp.get()
                for g in range(2):
                    ps = pso.get()
                    for kt in range(KT):
                        mm(ps[:, :], ygT[:, kt, :], w3[:, kt, D + g * 512:D + (g + 1) * 512], kt == 0, kt == KT - 1)
                    tt("dve", t_[:, g * 512:(g + 1) * 512], ps[:, :], gateB[:, row, g * 512:(g + 1) * 512], ALU.mult)
                z = zp.get()
                stt(z[:], xt[:], alpha, t_[:], ALU.mult, ALU.add)
                st = stp.get()
                for g in range(2):
                    r, w = _rw([st[:]], [z[:]])
                    S.op("dve", lambda h, st=st, z=z, g=g: h.bn_stats(out=st.ap[:, g * 6:(g + 1) * 6], in_=z.ap[:, g * 512:(g + 1) * 512]), r, w)
                r, w = _rw([st[:]], [st[:]])
                S.op("dve", lambda h, st=st: h.bn_aggr(out=st.ap[:, 12:14], in_=st.ap[:, 0:12]), r, w)
                act(st[:, 14:15], st[:, 13:14], AF.Ln, bias=LN_EPS)
                act(st[:, 14:15], st[:, 14:15], AF.Exp, scale=-0.5)
                stt(st[:, 15:16], st[:, 12:13], -1.0, st[:, 14:15], ALU.mult, ALU.mult)
                o = op_.get()
                act(o[:], z[:], AF.Identity, scale=st[:, 14:15], bias=st[:, 15:16])
                tt("pool", o[:], o[:], lngB[:], ALU.mult)
                tt("pool", o[:], o[:], lnbB[:], ALU.add)
                if l < NL - 1:
                    dma(Dr(X1[s, c * 128:(c + 1) * 128, :]), o[:])
                else:
                    dma(Dr(out_d[s, (c - 2) * 128:(c - 1) * 128, :]), o[:])
        S.barrier()

    for l in range(NL):
        if "p0" in phases:
            phase0(l)
        if "p1" in phases:
            phase1(l)
        if "a" in phases:
            phaseA(l)
        if "b" in phases:
            phaseB(l)
        if "c" in phases:
            phaseC(l)
        if "d" in phases:
            phaseD(l)
        if "p3" in phases:
            phase3(l)
    S.barrier()
    S.emit()
    return nc, S


_CACHE = {}


def make_in_maps(inputs, NS, NL, cores):
    f = lambda a: np.ascontiguousarray(np.asarray(a, dtype=np.float32))
    x, c, ctx, c_ctx = f(inputs["x"]), f(inputs["c"]), f(inputs["ctx"]), f(inputs["c_ctx"])
    w_in = f(inputs["w_in"])[:NL]
    fmc, tmc = fm_columns(), tm_columns()
    shared = dict(
        wfm=np.ascontiguousarray(w_in[:, :, fmc]),
        wtm=np.ascontiguousarray(w_in[:, :, tmc]),
        wg=np.ascontiguousarray(w_in[:, :, 2816:3840]),
        wout=f(inputs["w_out"])[:NL],
        wada=f(inputs["w_ada"])[:NL],
        bada=f(inputs["b_ada"])[:NL],
        lng=f(inputs["ln_g"])[:NL],
        lnb=f(inputs["ln_b"])[:NL],
        hgg=f(inputs["hg_norm_g"])[:NL],
        dfl=f(inputs["df_lambda"])[:NL].reshape(NL, 128),
        dfg=f(inputs["df_subln_g"])[:NL],
        snk=f(inputs["wn_sink"])[:NL],
        consts=const_table(),
        rope=rope_tables(),
    )
    lbl = f(inputs["hg_lb_logits"])
    shared["lbl"] = np.ascontiguousarray(lbl.reshape(2, 2, 2, 128).transpose(3, 0, 1, 2).reshape(128, 8))
    pidx = np.array([_partner(d, 64) for d in range(64)])
    gq, gk = f(inputs["ga_q_norm_g"])[:NL], f(inputs["ga_k_norm_g"])[:NL]
    gqk = np.stack([np.tile(gq, (1, 2)), np.tile(gq[:, pidx], (1, 2)),
                    np.tile(gk, (1, 2)), np.tile(gk[:, pidx], (1, 2))], axis=-1)
    shared["gqk"] = np.ascontiguousarray(gqk)
    maps = []
    for ci in cores:
        b0 = ci * NS
        rows = np.concatenate([c[b0:b0 + NS], c_ctx[None, :]], axis=0)
        cT = np.ascontiguousarray(rows.reshape(NS + 1, KT, 128).transpose(2, 1, 0))
        m = dict(shared)
        m["x"] = np.ascontiguousarray(x[b0:b0 + NS])
        m["ctx"] = np.ascontiguousarray(ctx[b0:b0 + NS])
        m["cT"] = cT
        maps.append(m)
    return maps


def kernel(**inputs):
    NS, NL, NCORE = 2, 2, 8
    GROUP = int(_os0.environ.get("K_GROUP", "2"))
    if "nc" not in _CACHE:
        _CACHE["nc"] = build(NS, NL)[0]
    nc = _CACHE["nc"]
    maps = make_in_maps(inputs, NS, NL, list(range(NCORE)))
    outs = []
    for g0 in range(0, NCORE, GROUP):
        res = run_bass_kernel_spmd(nc, maps[g0:g0 + GROUP], core_ids=list(range(GROUP)))
        outs.extend(np.asarray(r["out"], dtype=np.float32) for r in res.results)
    return np.concatenate(outs, axis=0)
```
